# Optimizing a Trainium2 kernel written in Bass

```python
import jax, jax.numpy as jnp
from jax import lax
import numpy as np

D_MODEL = 4096
BATCH = 2
SEQ = 8192
DEPTH = 2

GRID_W = 64
CTX_LEN = 256
HEAD_DIM = 128
BRANCH_W = D_MODEL // 4
N_BRANCH = 4
A_HEADS = BRANCH_W // HEAD_DIM
A_KV_HEADS = max(1, A_HEADS // 4)
A_WINDOW = 128
A_BLOCK = 128
NA_HEADS = BRANCH_W // HEAD_DIM
NA_ROWS = 8
NA_COLS = 16
NA_COL_BLOCK = 16
POOL_GROUPS = 4
POOL_G = BRANCH_W // POOL_GROUPS
POOL_WINDOWS = (2, 4, 8, 16)
FNET_GROUPS = 4
D_FF = ((8 * D_MODEL // 3 + 255) // 256) * 256
ROPE_BASE = 10000.0
EPS = 1e-6

A_Q_W = A_HEADS * HEAD_DIM
A_KV_W = A_KV_HEADS * HEAD_DIM
NA_W = NA_HEADS * HEAD_DIM
OFF_AK = A_Q_W
OFF_AV = OFF_AK + A_KV_W
OFF_NQ = OFF_AV + A_KV_W
OFF_NK = OFF_NQ + NA_W
OFF_NV = OFF_NK + NA_W
OFF_PU = OFF_NV + NA_W
OFF_FU = OFF_PU + BRANCH_W
OFF_GT = OFF_FU + BRANCH_W
IN_TOTAL = OFF_GT + N_BRANCH * D_MODEL
IN_SPLITS = (OFF_AK, OFF_AV, OFF_NQ, OFF_NK, OFF_NV, OFF_PU, OFF_FU, OFF_GT)

kernel_name = 'hybrid_gated_window_natten_pool_fnet_dit'

F32 = jnp.float32


def rms_norm(x, g):
    xf = x.astype(F32)
    y = xf * lax.rsqrt(jnp.mean(xf * xf, axis=-1, keepdims=True) + EPS)
    return (y * g.astype(F32)).astype(x.dtype)


def heads(u, n):
    return u.reshape(u.shape[0], u.shape[1], n, HEAD_DIM)


def axial_rope(x, rows, cols):
    half = x.shape[-1] // 2
    inv = jnp.power(ROPE_BASE, -jnp.arange(0, half, 2, dtype=F32) / half)

    def rot(xp, pos):
        ang = pos.astype(F32)[:, None] * inv[None, :]
        cos = jnp.cos(ang)[None, :, None, :].astype(x.dtype)
        sin = jnp.sin(ang)[None, :, None, :].astype(x.dtype)
        x1, x2 = xp[..., : half // 2], xp[..., half // 2:]
        return jnp.concatenate([x1 * cos - x2 * sin, x1 * sin + x2 * cos], axis=-1)

    return jnp.concatenate([rot(x[..., :half], rows), rot(x[..., half:], cols)], axis=-1)


def window_attention(q, k, v, ck, cv, sink):
    Bn, S, Hq, dh = q.shape
    Hkv = k.shape[2]
    G = Hq // Hkv
    L = ck.shape[1]
    nb = S // A_BLOCK
    scale = dh ** -0.5
    qb = q.reshape(Bn, nb, A_BLOCK, Hkv, G, dh)

    def band(z):
        zp = jnp.pad(z, ((0, 0), (A_BLOCK, A_BLOCK), (0, 0), (0, 0))).reshape(Bn, nb + 2, A_BLOCK, Hkv, dh)
        return jnp.concatenate([zp[:, :-2], zp[:, 1:-1], zp[:, 2:]], axis=2)

    kw, vw = band(k), band(v)
    qi = jnp.arange(A_BLOCK)[:, None]
    kj = jnp.arange(3 * A_BLOCK)[None, :]
    kpos = jnp.arange(nb)[:, None] * A_BLOCK - A_BLOCK + kj
    mask = (jnp.abs(kj - A_BLOCK - qi) <= A_WINDOW)[None] & ((kpos >= 0) & (kpos < S))[:, None, :]
    s_loc = jnp.einsum('bnqhgd,bnkhd->bhgnqk', qb, kw).astype(F32) * scale
    s_loc = jnp.where(mask, s_loc, -jnp.inf)
    s_ctx = jnp.einsum('bnqhgd,blhd->bhgnql', qb, ck).astype(F32) * scale
    s_sink = jnp.broadcast_to(sink.astype(F32).reshape(1, Hkv, G, 1, 1, 1), s_ctx.shape[:-1] + (1,))
    p = jax.nn.softmax(jnp.concatenate([s_loc, s_ctx, s_sink], axis=-1), axis=-1).astype(v.dtype)
    nloc = 3 * A_BLOCK
    o = (jnp.einsum('bhgnqk,bnkhd->bnqhgd', p[..., :nloc], vw)
         + jnp.einsum('bhgnql,blhd->bnqhgd', p[..., nloc:nloc + L], cv))
    return o.reshape(Bn, S, Hq * dh)


def neighbourhood_attention(q, k, v, ck, cv, rpb):
    Bn, S, H, dh = q.shape
    rows = S // GRID_W
    kr = min(NA_ROWS, rows)
    ncb = GRID_W // NA_COL_BLOCK
    halo = NA_COL_BLOCK + NA_COLS
    r = jnp.arange(rows)
    row_idx = jnp.clip(r - kr // 2, 0, rows - kr)[:, None] + jnp.arange(kr)[None, :]
    jb = jnp.arange(ncb)
    col_idx = jnp.clip(jb * NA_COL_BLOCK - NA_COLS // 2, 0, GRID_W - halo)[:, None] + jnp.arange(halo)[None, :]
    qcol = jb[:, None] * NA_COL_BLOCK + jnp.arange(NA_COL_BLOCK)[None, :]
    cstart = jnp.clip(qcol - NA_COLS // 2, 0, GRID_W - NA_COLS)
    kc = col_idx[:, None, :]
    in_win = (kc >= cstart[..., None]) & (kc < cstart[..., None] + NA_COLS)
    dr = row_idx - r[:, None] + NA_ROWS - 1
    dc = jnp.clip(kc - qcol[..., None] + NA_COLS - 1, 0, 2 * NA_COLS - 2)
    bias = rpb[:, dr[:, None, None, :, None], dc[None, :, :, None, :]].astype(F32)
    bias = jnp.where(in_win[None, None, :, :, None, :], bias, -jnp.inf)
    qg = q.reshape(Bn, rows, ncb, NA_COL_BLOCK, H, dh)
    ridx = row_idx[:, None, :, None]
    cidx = col_idx[None, :, None, :]
    kn = k.reshape(Bn, rows, GRID_W, H, dh)[:, ridx, cidx]
    vn = v.reshape(Bn, rows, GRID_W, H, dh)[:, ridx, cidx]
    scale = dh ** -0.5
    nloc = kr * halo
    s_loc = jnp.einsum('brjqhd,brjkchd->bhrjqkc', qg, kn).astype(F32) * scale + bias[None]
    s_loc = s_loc.reshape(Bn, H, rows, ncb, NA_COL_BLOCK, nloc)
    s_ctx = jnp.einsum('brjqhd,blhd->bhrjql', qg, ck).astype(F32) * scale
    p = jax.nn.softmax(jnp.concatenate([s_loc, s_ctx], axis=-1), axis=-1).astype(v.dtype)
    p_loc = p[..., :nloc].reshape(Bn, H, rows, ncb, NA_COL_BLOCK, kr, halo)
    o = (jnp.einsum('bhrjqkc,brjkchd->brjqhd', p_loc, vn)
         + jnp.einsum('bhrjql,blhd->brjqhd', p[..., nloc:], cv))
    return o.reshape(Bn, S, H * dh)


def context_attention(q, k, v, sink):
    Bn, L, Hq, dh = q.shape
    Hkv = k.shape[2]
    G = Hq // Hkv
    qg = q.reshape(Bn, L, Hkv, G, dh)
    s = jnp.einsum('blhgd,bmhd->bhglm', qg, k).astype(F32) * (dh ** -0.5)
    if sink is None:
        p = jax.nn.softmax(s, axis=-1)
    else:
        s_sink = jnp.broadcast_to(sink.astype(F32).reshape(1, Hkv, G, 1, 1), s.shape[:-1] + (1,))
        p = jax.nn.softmax(jnp.concatenate([s, s_sink], axis=-1), axis=-1)[..., :L]
    o = jnp.einsum('bhglm,bmhd->blhgd', p.astype(v.dtype), v)
    return o.reshape(Bn, L, Hq * dh)


def multiscale_pool(u, w_pool, scale):
    Bn, N, C = u.shape
    ug = u.astype(F32).reshape(Bn, N, POOL_GROUPS, POOL_G)
    csum = jnp.concatenate([jnp.zeros_like(ug[:, :1]), jnp.cumsum(ug, axis=1)], axis=1)
    t = jnp.arange(N)[:, None]
    win = jnp.array(POOL_WINDOWS, dtype=jnp.int32)[None, :]
    lo = jnp.clip(t - win // 2, 0, N)
    hi = jnp.clip(t - win // 2 + win, 0, N)
    grp = jnp.arange(POOL_GROUPS)[None, :]
    mean = (csum[:, hi, grp] - csum[:, lo, grp]) / (hi - lo).astype(F32)[None, :, :, None]
    y = jnp.einsum('bngc,gcd->bngd', (mean - ug).astype(u.dtype), w_pool)
    return y.reshape(Bn, N, C) * scale


def fourier_mix(u, w):
    Bn, N, C = u.shape
    ug = u.astype(F32).reshape(Bn, N, FNET_GROUPS, C // FNET_GROUPS).transpose(0, 2, 1, 3)
    y = jnp.fft.fft2(ug, norm='ortho').real
    y = y.transpose(0, 2, 1, 3).reshape(Bn, N, C).astype(u.dtype)
    return y @ w


def gated_merge(ys, gate_logits, w_branch, w_out):
    merged = None
    for i in range(N_BRANCH):
        g = jax.nn.sigmoid(gate_logits[..., i * D_MODEL:(i + 1) * D_MODEL])
        term = g * (ys[i] @ w_branch[i])
        merged = term if merged is None else merged + term
    return merged @ w_out


def conv_ffn(h, w_gate, w_val, conv_w, conv_b, w_down):
    a = h @ w_gate
    zero = jnp.zeros_like(a[:, :1])
    a = (jnp.concatenate([zero, a[:, :-1]], axis=1) * conv_w[0] + a * conv_w[1]
         + jnp.concatenate([a[:, 1:], zero], axis=1) * conv_w[2] + conv_b)
    return (jax.nn.silu(a) * (h @ w_val)) @ w_down


def setup_inputs(seed: int = 0) -> dict:
    key = jax.random.key(seed)
    ks = jax.random.split(key, 24)
    nrm = jax.random.normal
    D = D_MODEL
    return {
        'x': nrm(ks[0], (BATCH, SEQ, D), F32),
        'c': nrm(ks[1], (BATCH, D), F32),
        'ctx': nrm(ks[2], (BATCH, CTX_LEN, D), F32),
        'c_ctx': nrm(ks[3], (D,), F32),
        'w_mod': nrm(ks[4], (DEPTH, D, 6 * D), F32) * (0.5 * D ** -0.5),
        'b_mod': nrm(ks[5], (DEPTH, 6 * D), F32) * 0.02,
        'g_mix': 1.0 + 0.05 * nrm(ks[6], (DEPTH, D), F32),
        'w_in': nrm(ks[7], (DEPTH, D, IN_TOTAL), F32) * D ** -0.5,
        'a_sink': nrm(ks[8], (DEPTH, A_HEADS), F32),
        'na_rpb': 0.5 * nrm(ks[9], (DEPTH, NA_HEADS, 2 * NA_ROWS - 1, 2 * NA_COLS - 1), F32),
        'w_pool': nrm(ks[10], (DEPTH, POOL_GROUPS, POOL_G, POOL_G), F32) * POOL_G ** -0.5,
        'pool_scale': 1.0 + 0.1 * nrm(ks[11], (DEPTH, BRANCH_W), F32),
        'w_fnet': nrm(ks[12], (DEPTH, BRANCH_W, BRANCH_W), F32) * BRANCH_W ** -0.5,
        'w_branch': nrm(ks[13], (DEPTH, N_BRANCH, BRANCH_W, D), F32) * BRANCH_W ** -0.5,
        'w_out': nrm(ks[14], (DEPTH, D, D), F32) * D ** -0.5,
        'g_ffn': 1.0 + 0.05 * nrm(ks[15], (DEPTH, D), F32),
        'w_ff_gate': nrm(ks[16], (DEPTH, D, D_FF), F32) * D ** -0.5,
        'w_ff_val': nrm(ks[17], (DEPTH, D, D_FF), F32) * D ** -0.5,
        'ff_conv_w': nrm(ks[18], (DEPTH, 3, D_FF), F32) * 3 ** -0.5,
        'ff_conv_b': 0.02 * nrm(ks[19], (DEPTH, D_FF), F32),
        'w_ff_down': nrm(ks[20], (DEPTH, D_FF, D), F32) * D_FF ** -0.5,
        'g_final': 1.0 + 0.05 * nrm(ks[21], (D,), F32),
    }


def reference(x, c, ctx, c_ctx, w_mod, b_mod, g_mix, w_in, a_sink, na_rpb, w_pool, pool_scale, w_fnet,
              w_branch, w_out, g_ffn, w_ff_gate, w_ff_val, ff_conv_w, ff_conv_b, w_ff_down, g_final):
    S = x.shape[1]
    t = jnp.arange(S)
    grid_row, grid_col = t // GRID_W, t % GRID_W
    cond = jax.nn.silu(c)
    cond_ctx = jax.nn.silu(c_ctx)
    xc = ctx
    for l in range(DEPTH):
        last = l == DEPTH - 1
        sh1, sc1, gt1, sh2, sc2, gt2 = jnp.split((cond @ w_mod[l] + b_mod[l])[:, None, :], 6, axis=-1)
        csh1, csc1, cgt1, csh2, csc2, cgt2 = jnp.split(cond_ctx @ w_mod[l] + b_mod[l], 6, axis=-1)
        h = rms_norm(x, g_mix[l]) * (1 + sc1) + sh1
        hc = rms_norm(xc, g_mix[l]) * (1 + csc1) + csh1
        aq, ak, av, nq, nk, nv, pool_u, fnet_u, gate_l = jnp.split(h @ w_in[l], IN_SPLITS, axis=-1)
        if last:
            cak, cav = jnp.split(hc @ w_in[l][:, OFF_AK:OFF_NQ], 2, axis=-1)
            cnk, cnv = jnp.split(hc @ w_in[l][:, OFF_NK:OFF_PU], 2, axis=-1)
        else:
            caq, cak, cav, cnq, cnk, cnv, cpool_u, cfnet_u, cgate_l = jnp.split(hc @ w_in[l], IN_SPLITS, axis=-1)
        ya = window_attention(axial_rope(heads(aq, A_HEADS), grid_row, grid_col),
                              axial_rope(heads(ak, A_KV_HEADS), grid_row, grid_col),
                              heads(av, A_KV_HEADS), heads(cak, A_KV_HEADS), heads(cav, A_KV_HEADS), a_sink[l])
        yn = neighbourhood_attention(heads(nq, NA_HEADS), heads(nk, NA_HEADS), heads(nv, NA_HEADS),
                                     heads(cnk, NA_HEADS), heads(cnv, NA_HEADS), na_rpb[l])
        yp = multiscale_pool(pool_u, w_pool[l], pool_scale[l])
        yf = fourier_mix(fnet_u, w_fnet[l])
        x = x + gt1 * gated_merge((ya, yn, yp, yf), gate_l, w_branch[l], w_out[l])
        h2 = rms_norm(x, g_ffn[l]) * (1 + sc2) + sh2
        x = x + gt2 * conv_ffn(h2, w_ff_gate[l], w_ff_val[l], ff_conv_w[l], ff_conv_b[l], w_ff_down[l])
        if not last:
            cya = context_attention(heads(caq, A_HEADS), heads(cak, A_KV_HEADS), heads(cav, A_KV_HEADS), a_sink[l])
            cyn = context_attention(heads(cnq, NA_HEADS), heads(cnk, NA_HEADS), heads(cnv, NA_HEADS), None)
            cyp = multiscale_pool(cpool_u, w_pool[l], pool_scale[l])
            cyf = fourier_mix(cfnet_u, w_fnet[l])
            xc = xc + cgt1 * gated_merge((cya, cyn, cyp, cyf), cgate_l, w_branch[l], w_out[l])
            hc2 = rms_norm(xc, g_ffn[l]) * (1 + csc2) + csh2
            xc = xc + cgt2 * conv_ffn(hc2, w_ff_gate[l], w_ff_val[l], ff_conv_w[l], ff_conv_b[l], w_ff_down[l])
    return rms_norm(x, g_final)
```

```python
import contextlib
import numpy as np
import ml_dtypes
import concourse.bass as bass
import concourse.mybir as mybir
from concourse.bass_utils import run_bass_kernel_spmd

F32 = mybir.dt.float32
BF16 = mybir.dt.bfloat16
AF = mybir.ActivationFunctionType
ALU = mybir.AluOpType
AX = mybir.AxisListType
POOL_WINDOWS = (2, 4, 8, 16)
NEG = -30000.0


class Sched:
    def __init__(self, nc, stack):
        self.nc = nc
        self.stack = stack
        self.eng = {'pe': nc.tensor, 'dve': nc.vector, 'act': nc.scalar, 'pool': nc.gpsimd, 'sp': nc.sync}
        self.sems = {}
        self.val = {}
        self.known = {e: {} for e in self.eng}
        self.last_w = {}
        self.rd = {}

    def sem(self, key):
        if key not in self.sems:
            self.sems[key] = self.stack.enter_context(self.nc.semaphore("s%d" % len(self.sems)))
            self.val[key] = 0
        return self.sems[key]

    def _wait(self, e, tok):
        key, v = tok
        if self.known[e].get(key, 0) >= v:
            return
        self.known[e][key] = v
        self.eng[e].wait_ge(self.sem(key), v)

    def op(self, e, fn, reads=(), writes=(), dma=None):
        toks = []
        for r in reads:
            t = self.last_w.get(r)
            if t is not None:
                toks.append(t)
        for w in writes:
            t = self.last_w.get(w)
            if t is not None:
                toks.append(t)
            d = self.rd.get(w)
            if d:
                toks.extend(d.items())
        for t in toks:
            if e == 'pe' and t[0] == ('E', 'pe'):
                continue
            self._wait(e, t)
        ins = fn(self.eng[e])
        key = ('E', e) if dma is None else ('D', dma)
        inc = 1 if dma is None else 16
        self.sem(key)
        self.val[key] += inc
        ins.then_inc(self.sems[key], inc)
        tok = (key, self.val[key])
        for r in reads:
            d = self.rd.setdefault(r, {})
            d[key] = max(d.get(key, 0), tok[1])
        for w in writes:
            self.last_w[w] = tok
            self.rd[w] = {}
        return tok

    def chain(self, e, fns, reads=(), writes=()):
        toks = []
        for r in reads:
            t = self.last_w.get(r)
            if t is not None:
                toks.append(t)
        for w in writes:
            t = self.last_w.get(w)
            if t is not None:
                toks.append(t)
            d = self.rd.get(w)
            if d:
                toks.extend(d.items())
        for t in toks:
            if e == 'pe' and t[0] == ('E', 'pe'):
                continue
            self._wait(e, t)
        ins = None
        for fn in fns:
            ins = fn(self.eng[e])
        key = ('E', e)
        self.sem(key)
        self.val[key] += 1
        ins.then_inc(self.sems[key], 1)
        tok = (key, self.val[key])
        for r in reads:
            d = self.rd.setdefault(r, {})
            d[key] = max(d.get(key, 0), tok[1])
        for w in writes:
            self.last_w[w] = tok
            self.rd[w] = {}
        return tok

    def barrier(self):
        toks = [(k, v) for k, v in self.val.items() if v > 0]
        for e in self.eng:
            for t in toks:
                self._wait(e, t)
        self.last_w = {}
        self.rd = {}


_UN = [0]


def un(name):
    _UN[0] += 1
    return "%s_u%d" % (name, _UN[0])


class Ring:
    def __init__(self, nc, st, name, shape, dtype, n, psum=False):
        mk = nc.psum_tensor if psum else nc.sbuf_tensor
        self.t = [st.enter_context(mk(un("%s%d" % (name, i)), shape, dtype)) for i in range(n)]
        self.k = [(name, i) for i in range(n)]
        self.i = 0

    def next(self):
        i = self.i
        self.i = (i + 1) % len(self.t)
        return self.t[i], self.k[i]


def make_cfg(D=4096, SQ=8192, L=256, DEPTH=2, NB=1):
    c = dict(D=D, SQ=SQ, L=L, DEPTH=DEPTH, NB=NB, GW=64)
    c['KC'] = D // 128
    BW = D // 4
    c['BW'] = BW
    c['BC'] = BW // 128
    c['NH'] = BW // 128
    c['HKV'] = max(1, c['NH'] // 4)
    c['G'] = c['NH'] // c['HKV']
    c['AKV'] = c['HKV'] * 128
    c['PG'] = BW // 4
    c['PGC'] = c['PG'] // 128
    c['DFF'] = ((8 * D // 3 + 255) // 256) * 256
    c['FC'] = c['DFF'] // 128
    c['ST'] = SQ + L
    o = {}
    o['AQ'] = 0
    o['AK'] = BW
    o['AV'] = o['AK'] + c['AKV']
    o['NQ'] = o['AV'] + c['AKV']
    o['NK'] = o['NQ'] + BW
    o['NV'] = o['NK'] + BW
    o['PU'] = o['NV'] + BW
    o['FU'] = o['PU'] + BW
    o['GT'] = o['FU'] + BW
    c['off'] = o
    c['IN_TOTAL'] = o['GT'] + 4 * D
    c['ROWS'] = SQ // 64
    return c


def na_tiles(cfg):
    R = cfg['ROWS']
    kr = min(8, R)
    variants = []
    tiles = []
    for r in range(0, R, 2):
        base = min(max(r - 4, 0), R - 9)
        s0 = min(max(r - kr // 2, 0), R - kr) - base
        s1 = min(max(r + 1 - kr // 2, 0), R - kr) - base
        key = (s0, s1)
        if key not in variants:
            variants.append(key)
        tiles.append((base, variants.index(key)))
    return tiles, variants, kr


def host_consts(cfg, na_rpb):
    D, SQ, L, DEPTH = cfg['D'], cfg['SQ'], cfg['L'], cfg['DEPTH']
    NH = cfg['NH']
    out = {}
    out['ident'] = np.eye(128, dtype=np.float32)
    out['identb'] = np.eye(128).astype(ml_dtypes.bfloat16)
    out['ones'] = np.ones((128, 128), np.float32)
    i = np.arange(128)[:, None]
    j = np.arange(128)[None, :]
    m = np.zeros((128, 384), np.float32)
    m[:, 0:128] = np.where(j >= i, 0.0, NEG)
    m[:, 256:384] = np.where(j <= i, 0.0, NEG)
    out['wmask'] = m
    t = np.arange(SQ)
    row, col = t // 64, t % 64
    inv = np.power(10000.0, -np.arange(0, 64, 2, dtype=np.float64) / 64)
    d = np.arange(128)
    pos = np.where(d[:, None] < 64, row[None, :], col[None, :]).astype(np.float64)
    ang = pos * inv[d % 32][:, None]
    out['ropec'] = np.cos(ang.astype(np.float32)).astype(np.float32)
    out['ropes'] = np.sin(ang.astype(np.float32)).astype(np.float32)
    P = np.zeros((128, 128), np.float32)
    for dd in range(128):
        if (dd % 64) < 32:
            P[dd + 32, dd] = -1.0
        else:
            P[dd - 32, dd] = 1.0
    out['rperm'] = P
    def invc(N):
        tt = np.arange(N)
        r = np.zeros((4, N), np.float32)
        for g, w in enumerate(POOL_WINDOWS):
            lo = np.clip(tt - w // 2, 0, N)
            hi = np.clip(tt - w // 2 + w, 0, N)
            r[g] = 1.0 / (hi - lo)
        return np.ascontiguousarray(np.broadcast_to(r[:, None, :], (4, 128, N))).astype(np.float32)
    out['invcnt'] = invc(SQ)
    out['invcntc'] = invc(L)
    def dft(N):
        k = np.arange(N, dtype=np.int64)
        a = (2 * np.pi / N) * ((k[:, None] * k[None, :]) % N)
        return np.cos(a).astype(ml_dtypes.bfloat16), np.sin(a).astype(ml_dtypes.bfloat16)
    out['dftc'], out['dfts'] = dft(SQ)
    out['dftcc'], out['dftsc'] = dft(L)
    out['dftgc'], out['dftgs'] = dft(cfg['PG'])
    tiles, variants, kr = na_tiles(cfg)
    nb = np.full((DEPTH, len(variants), NH, 128, 576), NEG, np.float32)
    q = np.arange(128)
    qr, qc = q // 64, q % 64
    k = np.arange(576)
    krow, kcol = k // 64, k % 64
    cstart = np.clip(qc - 8, 0, 64 - 16)
    for vi, (s0, s1) in enumerate(variants):
        rs = np.where(qr == 0, s0, s1)
        inrow = (krow[None, :] >= rs[:, None]) & (krow[None, :] < rs[:, None] + kr)
        incol = (kcol[None, :] >= cstart[:, None]) & (kcol[None, :] < cstart[:, None] + 16)
        nb_v = None
        for li in range(DEPTH):
            pass
        out.setdefault('_na_masks', []).append((inrow & incol, rs))
    out['_na_variants'] = variants
    out['_na_tiles'] = tiles
    return out, nb


def build_nabias(cfg, na_rpb):
    R = cfg['ROWS']
    DEPTH, NH = cfg['DEPTH'], cfg['NH']
    kr = min(8, R)
    tiles = []
    vkeys = []
    q = np.arange(128)
    qr, qc = q // 64, q % 64
    k = np.arange(576)
    krow, kcol = k // 64, k % 64
    cstart = np.clip(qc - 8, 0, 64 - 16)
    arrs = []
    for r in range(0, R, 2):
        base = min(max(r - 4, 0), R - 9)
        rq = r + qr
        rs = np.clip(rq - kr // 2, 0, R - kr)
        rk = base + krow
        key = (r - base, tuple(rs - base))
        if key not in vkeys:
            vkeys.append(key)
            inrow = (rk[None, :] >= rs[:, None]) & (rk[None, :] < rs[:, None] + kr)
            incol = (kcol[None, :] >= cstart[:, None]) & (kcol[None, :] < cstart[:, None] + 16)
            dr = np.clip(rk[None, :] - rq[:, None] + 7, 0, 14)
            dc = np.clip(kcol[None, :] - qc[:, None] + 15, 0, 30)
            g = na_rpb[:, :, dr, dc]
            arrs.append(np.where((inrow & incol)[None, None], g, np.float32(NEG)).astype(np.float32))
        tiles.append((base, vkeys.index(key)))
    return np.ascontiguousarray(np.stack(arrs, axis=1)), tiles


def build(cfg, na_tile_info, n_var):
    D, SQ, L, DEPTH, NB = cfg['D'], cfg['SQ'], cfg['L'], cfg['DEPTH'], cfg['NB']
    KC, BW, BC, NH, HKV, G, AKV = cfg['KC'], cfg['BW'], cfg['BC'], cfg['NH'], cfg['HKV'], cfg['G'], cfg['AKV']
    PG, PGC, DFF, FC, ST = cfg['PG'], cfg['PGC'], cfg['DFF'], cfg['FC'], cfg['ST']
    off = cfg['off']
    INT = cfg['IN_TOTAL']
    R = NB + 1
    scale = 128 ** -0.5
    nc = bass.Bass("TRN2", target_bir_lowering=False)

    def din(name, shape, dt=F32):
        return nc.dram_tensor(name, list(shape), dt, kind="ExternalInput").ap()

    def dscr(name, shape, dt):
        return nc.dram_tensor(name, list(shape), dt, kind="Internal").ap()

    x_in = din("x", [NB * SQ, D])
    c_in = din("c", [NB, D])
    ctx_in = din("ctx", [NB * L, D])
    cctx_in = din("c_ctx", [1, D])
    w_mod = din("w_mod", [DEPTH * D, 6 * D])
    b_mod = din("b_mod", [DEPTH, 6 * D])
    g_mix = din("g_mix", [DEPTH, D])
    w_in = din("w_in", [DEPTH * D, INT])
    a_sink = din("a_sink", [DEPTH, NH])
    nabias = din("nabias", [DEPTH * n_var * NH * 128, 576])
    w_pool = din("w_pool", [DEPTH * 4 * PG, PG])
    pool_scale = din("pool_scale", [DEPTH, BW])
    w_fnet = din("w_fnet", [DEPTH * BW, BW])
    w_branch = din("w_branch", [DEPTH * 4 * BW, D])
    w_out = din("w_out", [DEPTH * D, D])
    g_ffn = din("g_ffn", [DEPTH, D])
    w_fg = din("w_ff_gate", [DEPTH * D, DFF])
    w_fv = din("w_ff_val", [DEPTH * D, DFF])
    conv_w = din("ff_conv_w", [DEPTH * 3, DFF])
    conv_b = din("ff_conv_b", [DEPTH, DFF])
    w_fd = din("w_ff_down", [DEPTH * DFF, D])
    g_final = din("g_final", [1, D])
    ident_d = din("ident", [128, 128])
    identb_d = din("identb", [128, 128], BF16)
    ones_d = din("ones", [128, 128])
    wmask_d = din("wmask", [128, 384])
    ropec_d = din("ropec", [128, SQ])
    ropes_d = din("ropes", [128, SQ])
    rperm_d = din("rperm", [128, 128])
    invcnt_d = din("invcnt", [4 * 128, SQ])
    invcntc_d = din("invcntc", [4 * 128, L])
    dft_d = {('c', SQ): din("dftc", [SQ, SQ], BF16), ('s', SQ): din("dfts", [SQ, SQ], BF16),
             ('c', L): din("dftcc", [L, L], BF16), ('s', L): din("dftsc", [L, L], BF16)}
    dftgc_d = din("dftgc", [PG, PG], BF16)
    dftgs_d = din("dftgs", [PG, PG], BF16)
    y_out = nc.dram_tensor("y", [NB * SQ, D], F32, kind="ExternalOutput").ap()

    xT = dscr("xT", [D, ST], F32)
    hT = dscr("hT", [D, ST], BF16)
    qT = dscr("qT", [BW, ST], BF16)
    kT = dscr("kT", [AKV, ST], BF16)
    vA = dscr("vA", [ST, AKV], BF16)
    nqT = dscr("nqT", [BW, ST], BF16)
    nkT = dscr("nkT", [BW, ST], BF16)
    vN = dscr("vN", [ST, BW], BF16)
    puT = dscr("puT", [BW, ST], F32)
    fuM = dscr("fuM", [ST, BW], BF16)
    yfT = dscr("yfT", [BW, ST], BF16)
    ysT = [dscr("ysT%d" % i, [BW, ST], BF16) for i in range(4)]
    mT = dscr("mT", [D, ST], BF16)
    pT = dscr("pT", [DFF, ST], BF16)

    with contextlib.ExitStack() as st:
        S = Sched(nc, st)

        def sb(name, shape, dt=F32):
            return st.enter_context(nc.sbuf_tensor(un(name), list(shape), dt))
        ident = sb("ident", [128, 128])
        identb = sb("identb", [128, 128], BF16)
        ones = sb("ones", [128, 128])
        rperm = sb("rperm", [128, 128])
        wmask = sb("wmask", [128, 384])
        mod_sb = sb("mod_sb", [128, 6 * KC, R])
        G_sb = sb("G_sb", [128, 2, KC, R])
        vec_sb = sb("vec_sb", [128, 2 * KC + 1])
        gfin_sb = sb("gfin_sb", [128, KC])
        bmod_sb = sb("bmod_sb", [128, 6 * KC])
        convw_sb = sb("convw_sb", [128, 4, FC])
        pscale_sb = sb("pscale_sb", [128, BC])
        sink_sb = sb("sink_sb", [128, NH])
        condT = sb("condT", [128, KC, R], BF16)
        psr = Ring(nc, st, "ps", [128, 512], F32, 6, psum=True)
        pst = Ring(nc, st, "pst", [128, 1024], BF16, 2, psum=True)
        wring = Ring(nc, st, "wr", [128, KC * 128], BF16, 4)
        wbig = Ring(nc, st, "wb", [128, FC * 128], BF16, 2)

        def ld(dst, src, writes, stream, reads=(), eng='sp', **kw):
            return S.op(eng, lambda g: g.dma_start(out=dst, in_=src, **kw), reads=reads, writes=writes, dma=stream)

        cnt = [0]

        def uid():
            cnt[0] += 1
            return cnt[0]

        for (dst, src, nm) in ((ident, ident_d, 'ident'), (identb, identb_d, 'identb'), (ones, ones_d, 'ones'),
                               (rperm, rperm_d, 'rperm'), (wmask, wmask_d, 'wmask')):
            ld(dst[:], src[:, :], [nm], 'c_' + nm)

        def ld_vec(dst, src_row, n, key):
            v = src_row.rearrange("o (c p) -> p (o c)", p=128)
            for c0 in range(0, n, 16):
                c1 = min(n, c0 + 16)
                ld(dst[:, c0:c1], v[:, c0:c1], [key], 'v_' + str(key), allow_slow_non_contiguous=True)

        ld_vec(gfin_sb[:, :], g_final[0:1, :], KC, 'gfin')

        Wt = {}

        def cast_w(name, src, K, N):
            kcw = K // 128
            nj = N // 128
            dst = dscr("Wt_" + name, [nj * 128, kcw * 128], BF16)
            dv = dst.rearrange("(j p) (k n) -> j p k n", p=128, n=128)
            sv = src.rearrange("(k p) n -> p k n", p=128)
            for j in range(nj):
                for k0 in range(0, kcw, 32):
                    k1 = min(kcw, k0 + 32)
                    ld(dv[j, :, k0:k1, :], sv[:, k0:k1, j * 128:(j + 1) * 128], [('W', name)], 'W_' + name, eng='pool')
            Wt[name] = (dv, kcw)

        for l in range(DEPTH):
            cast_w("mod%d" % l, w_mod[l * D:(l + 1) * D, :], D, 6 * D)
            cast_w("in%d" % l, w_in[l * D:(l + 1) * D, :], D, INT)
            for g in range(4):
                cast_w("pool%d_%d" % (l, g), w_pool[(l * 4 + g) * PG:(l * 4 + g + 1) * PG, :], PG, PG)
            cast_w("fnet%d" % l, w_fnet[l * BW:(l + 1) * BW, :], BW, BW)
            for i in range(4):
                cast_w("br%d_%d" % (l, i), w_branch[(l * 4 + i) * BW:(l * 4 + i + 1) * BW, :], BW, D)
            cast_w("out%d" % l, w_out[l * D:(l + 1) * D, :], D, D)
            cast_w("fg%d" % l, w_fg[l * D:(l + 1) * D, :], D, DFF)
            cast_w("fv%d" % l, w_fv[l * D:(l + 1) * D, :], D, DFF)
            cast_w("fd%d" % l, w_fd[l * DFF:(l + 1) * DFF, :], DFF, D)

        def load_w(name, j):
            dv, kcw = Wt[name]
            t, k = (wbig if kcw > KC else wring).next()
            ld(t[:, 0:kcw * 128], dv[j].rearrange("p k n -> p (k n)"), [k], '%s%d' % k, reads=[('W', name)])
            return t, k, kcw

        def segs(last):
            r = [(0, SQ, 'lat')]
            if not last:
                r.append((SQ, L, 'ctx'))
            return r

        def col_tiles(c0, n, T):
            return [(c, min(T, c0 + n - c)) for c in range(c0, c0 + n, T)]

        def fm_linear(wname, jlist, xs, xkey, tiles, epi):
            nxt = load_w(wname, jlist[0])
            for ji, j in enumerate(jlist):
                w, wk, kcw = nxt
                if ji + 1 < len(jlist):
                    nxt = load_w(wname, jlist[ji + 1])
                for ti, (lc, T) in enumerate(tiles):
                    ps, pk = psr.next()
                    S.chain('pe', [lambda p, kc=kc: p.matmul(ps[:, 0:T], w[:, kc * 128:(kc + 1) * 128], xs[:, kc, lc:lc + T],
                                                             start=(kc == 0), stop=(kc == kcw - 1)) for kc in range(kcw)],
                            reads=[wk, xkey], writes=[pk])
                    epi(ji, j, ti, T, ps, pk)

        def tm_linear(wname, jlist, xs, xkey, ntok, epi):
            nxt = load_w(wname, jlist[0])
            for ji, j in enumerate(jlist):
                w, wk, kcw = nxt
                if ji + 1 < len(jlist):
                    nxt = load_w(wname, jlist[ji + 1])
                for si in range(ntok // 128):
                    ps, pk = psr.next()
                    S.chain('pe', [lambda p, kc=kc: p.matmul(ps[:, 0:128], xs[:, kc, si * 128:(si + 1) * 128], w[:, kc * 128:(kc + 1) * 128],
                                                             start=(kc == 0), stop=(kc == kcw - 1)) for kc in range(kcw)],
                            reads=[wk, xkey], writes=[pk])
                    epi(ji, j, si, ps, pk)

        def norm_phase(b, which, last, final=False):
            with contextlib.ExitStack() as ph:
                T = 128
                xt_r = Ring(nc, ph, "nx", [128, KC, T], F32, 2)
                sq = ph.enter_context(nc.sbuf_tensor(un("nsq"), [128, KC, T], F32))
                rs = ph.enter_context(nc.sbuf_tensor(un("nrs"), [128, T], F32))
                ho_r = Ring(nc, ph, "nh", [128, KC, T], BF16 if not final else F32, 1 if final else 2)
                oy_r = Ring(nc, ph, "noy", [128, D], F32, 2) if final else None
                seglist = [(0, SQ, 'lat')] if final else [(0, SQ, 'lat'), (SQ, L, 'ctx')]
                for (s0, n, kind) in seglist:
                    row = b if kind == 'lat' else NB
                    for (c0, Tt) in col_tiles(s0, n, T):
                        xt, xk = xt_r.next()
                        ld(xt[:, :, 0:Tt], xT[:, c0:c0 + Tt].rearrange("(k p) t -> p k t", p=128), [xk], 'nx%d' % xk[1])
                        S.op('act', lambda a: a.activation(out=sq[:, :, 0:Tt], in_=xt[:, :, 0:Tt], func=AF.Square), reads=[xk], writes=['nsq'])
                        ps, pk = psr.next()
                        for kc in range(KC):
                            S.op('pe', lambda p, kc=kc: p.matmul(ps[:, 0:Tt], ones[:, :], sq[:, kc, 0:Tt], start=(kc == 0), stop=(kc == KC - 1)),
                                 reads=['nsq', 'ones'], writes=[pk])
                        S.op('dve', lambda v: v.tensor_scalar(out=rs[:, 0:Tt], in0=ps[:, 0:Tt], scalar1=1.0 / D, scalar2=1e-6, op0=ALU.mult, op1=ALU.add),
                             reads=[pk], writes=['nrs'])
                        S.op('act', lambda a: a.activation(out=rs[:, 0:Tt], in_=rs[:, 0:Tt], func=AF.Sqrt), reads=['nrs'], writes=['nrs'])
                        S.op('dve', lambda v: v.reciprocal(out=rs[:, 0:Tt], in_=rs[:, 0:Tt]), reads=['nrs'], writes=['nrs'])
                        ho, hk = ho_r.next()
                        for kc in range(KC):
                            S.op('pool', lambda v, kc=kc: v.tensor_tensor(out=sq[:, kc, 0:Tt], in0=xt[:, kc, 0:Tt], in1=rs[:, 0:Tt], op=ALU.mult),
                                 reads=[xk, 'nrs'], writes=['nsq'])
                            if final:
                                S.op('dve', lambda v, kc=kc: v.tensor_scalar(out=ho[:, kc, 0:Tt], in0=sq[:, kc, 0:Tt], scalar1=gfin_sb[:, kc:kc + 1], scalar2=None, op0=ALU.mult),
                                     reads=['nsq', 'gfin'], writes=[hk])
                            else:
                                S.op('dve', lambda v, kc=kc: v.tensor_scalar(out=ho[:, kc, 0:Tt], in0=sq[:, kc, 0:Tt], scalar1=G_sb[:, which, kc, row:row + 1],
                                                                             scalar2=mod_sb[:, (3 * which) * KC + kc, row:row + 1], op0=ALU.mult, op1=ALU.add),
                                     reads=['nsq', 'G', 'mod'], writes=[hk])
                        if not final:
                            ld(hT[:, c0:c0 + Tt].rearrange("(k p) t -> p k t", p=128), ho[:, :, 0:Tt], [], 'nst', reads=[hk])
                        else:
                            oy, ok = oy_r.next()
                            for kc in range(KC):
                                ps2, pk2 = psr.next()
                                S.op('pe', lambda p, kc=kc: p.transpose(ps2[:, 0:128], ho[:, kc, 0:128], ident[:, :]), reads=[hk, 'ident'], writes=[pk2])
                                e = 'act' if kc % 2 else 'dve'
                                if e == 'act':
                                    S.op('act', lambda a, kc=kc: a.copy(out=oy[:, kc * 128:(kc + 1) * 128], in_=ps2[:, 0:128]), reads=[pk2], writes=[ok])
                                else:
                                    S.op('dve', lambda v, kc=kc: v.tensor_copy(out=oy[:, kc * 128:(kc + 1) * 128], in_=ps2[:, 0:128]), reads=[pk2], writes=[ok])
                            ld(y_out[b * SQ + c0:b * SQ + c0 + 128, :], oy[:, :], [], 'yst', reads=[ok])
                S.barrier()

        def load_x_phase(b):
            with contextlib.ExitStack() as ph:
                xi_r = Ring(nc, ph, "xi", [128, D], F32, 2)
                xo_r = Ring(nc, ph, "xo", [128, KC, 128], F32, 2)
                for (src, r0, n, c0) in ((x_in, b * SQ, SQ, 0), (ctx_in, b * L, L, SQ)):
                    for t0 in range(0, n, 128):
                        xi, ik = xi_r.next()
                        ld(xi[:, :], src[r0 + t0:r0 + t0 + 128, :], [ik], 'xi%d' % ik[1])
                        xo, okk = xo_r.next()
                        for kc in range(KC):
                            ps, pk = psr.next()
                            S.op('pe', lambda p, kc=kc: p.transpose(ps[:, 0:128], xi[:, kc * 128:(kc + 1) * 128], ident[:, :]), reads=[ik, 'ident'], writes=[pk])
                            if kc % 2:
                                S.op('act', lambda a, kc=kc: a.copy(out=xo[:, kc, :], in_=ps[:, 0:128]), reads=[pk], writes=[okk])
                            else:
                                S.op('dve', lambda v, kc=kc: v.tensor_copy(out=xo[:, kc, :], in_=ps[:, 0:128]), reads=[pk], writes=[okk])
                        ld(xT[:, c0 + t0:c0 + t0 + 128].rearrange("(k p) t -> p k t", p=128), xo[:, :, :], [], 'xst', reads=[okk])
                S.barrier()

        def mod_phase(l):
            with contextlib.ExitStack() as ph:
                crow = ph.enter_context(nc.sbuf_tensor(un("crow"), [R, D], F32))
                ld(crow[0:NB, :], c_in[:, :], ['crow'], 'crow')
                ld(crow[NB:R, :], cctx_in[:, :], ['crow'], 'crow')
                S.op('act', lambda a: a.activation(out=crow[:, :], in_=crow[:, :], func=AF.Silu), reads=['crow'], writes=['crow'])
                for kc in range(KC):
                    ps, pk = psr.next()
                    S.op('pe', lambda p, kc=kc: p.transpose(ps[:, 0:R], crow[0:R, kc * 128:(kc + 1) * 128], ident[0:R, 0:R]), reads=['crow', 'ident'], writes=[pk])
                    S.op('dve', lambda v, kc=kc: v.tensor_copy(out=condT[:, kc, :], in_=ps[:, 0:R]), reads=[pk], writes=['condT'])
                ld_vec(bmod_sb[:, :], b_mod[l:l + 1, :], 6 * KC, 'bmod')
                ld_vec(vec_sb[:, 0:KC], g_mix[l:l + 1, :], KC, 'vec')
                ld_vec(vec_sb[:, KC:2 * KC], g_ffn[l:l + 1, :], KC, 'vec')
                for r in range(3):
                    ld_vec(convw_sb[:, r, :], conv_w[l * 3 + r:l * 3 + r + 1, :], FC, 'convw')
                ld_vec(convw_sb[:, 3, :], conv_b[l:l + 1, :], FC, 'convw')
                ld_vec(pscale_sb[:, :], pool_scale[l:l + 1, :], BC, 'pscale')
                for h in range(NH):
                    ld(sink_sb[:, h:h + 1], a_sink[l:l + 1, h:h + 1].partition_broadcast(128), ['sink'], 'sink')

                def epi(ji, j, ti, T, ps, pk):
                    S.op('dve', lambda v: v.tensor_scalar(out=mod_sb[:, j, :], in0=ps[:, 0:R], scalar1=bmod_sb[:, j:j + 1], scalar2=None, op0=ALU.add),
                         reads=[pk, 'bmod'], writes=['mod'])
                fm_linear("mod%d" % l, list(range(6 * KC)), condT, 'condT', [(0, R)], epi)
                for which in range(2):
                    for r in range(R):
                        S.op('dve', lambda v, which=which, r=r: v.scalar_tensor_tensor(
                            out=G_sb[:, which, :, r], in0=mod_sb[:, (3 * which + 1) * KC:(3 * which + 2) * KC, r], scalar=1.0,
                            in1=vec_sb[:, which * KC:(which + 1) * KC], op0=ALU.add, op1=ALU.mult), reads=['mod', 'vec'], writes=['G'])
                S.barrier()

        def inproj_phase(l, b, last):
            with contextlib.ExitStack() as ph:
                TB = 512
                xs = ph.enter_context(nc.sbuf_tensor(un("ixs"), [128, KC, TB], BF16))
                og_r = Ring(nc, ph, "iog", [128, 512], BF16, 3)
                of_r = Ring(nc, ph, "iof", [128, 512], F32, 2)
                xf_r = Ring(nc, ph, "ixf", [128, 512], F32, 2)
                t1_r = Ring(nc, ph, "it1", [128, 512], F32, 2)
                rc_r = Ring(nc, ph, "irc", [128, 2, TB], F32, 1)
                wn = "in%d" % l
                for (s0, n, kind) in [(0, SQ, 'lat'), (SQ, L, 'ctx')]:
                    for (b0, Tb) in col_tiles(s0, n, TB):
                        ld(xs[:, :, 0:Tb], hT[:, b0:b0 + Tb].rearrange("(k p) t -> p k t", p=128), ['ixs'], 'ixs')
                        tiles = col_tiles(0, Tb, 512)
                        if kind == 'lat':
                            rc, rk = rc_r.next()
                            ld(rc[:, 0, 0:Tb], ropec_d[:, b0:b0 + Tb], [rk], 'irc')
                            ld(rc[:, 1, 0:Tb], ropes_d[:, b0:b0 + Tb], [rk], 'irc')

                        def store(dst_ap, src_ap, key):
                            ld(dst_ap, src_ap, [], 'ist', reads=[key])

                        def epi_plain(dstT, joff, f32=False):
                            def epi(ji, j, ti, T, ps, pk):
                                lc = tiles[ti][0]
                                o, ok = (of_r if f32 else og_r).next()
                                if ji % 2:
                                    S.op('act', lambda a: a.copy(out=o[:, 0:T], in_=ps[:, 0:T]), reads=[pk], writes=[ok])
                                else:
                                    S.op('dve', lambda v: v.tensor_copy(out=o[:, 0:T], in_=ps[:, 0:T]), reads=[pk], writes=[ok])
                                store(dstT[(j - joff) * 128:(j - joff + 1) * 128, b0 + lc:b0 + lc + T], o[:, 0:T], ok)
                            return epi

                        def epi_rope(dstT, joff):
                            def epi(ji, j, ti, T, ps, pk):
                                lc = tiles[ti][0]
                                xf, xfk = xf_r.next()
                                S.op('act', lambda a: a.copy(out=xf[:, 0:T], in_=ps[:, 0:T]), reads=[pk], writes=[xfk])
                                ps2, pk2 = psr.next()
                                S.op('pe', lambda p: p.matmul(ps2[:, 0:T], rperm[:, :], xf[:, 0:T], start=True, stop=True), reads=[xfk, 'rperm'], writes=[pk2])
                                t1, t1k = t1_r.next()
                                S.op('dve', lambda v: v.tensor_tensor(out=t1[:, 0:T], in0=ps2[:, 0:T], in1=rc[:, 1, lc:lc + T], op=ALU.mult), reads=[pk2, rk], writes=[t1k])
                                S.op('pool', lambda v: v.tensor_tensor(out=xf[:, 0:T], in0=xf[:, 0:T], in1=rc[:, 0, lc:lc + T], op=ALU.mult), reads=[xfk, rk], writes=[xfk])
                                o, ok = og_r.next()
                                S.op('dve', lambda v: v.tensor_tensor(out=o[:, 0:T], in0=xf[:, 0:T], in1=t1[:, 0:T], op=ALU.add), reads=[xfk, t1k], writes=[ok])
                                store(dstT[(j - joff) * 128:(j - joff + 1) * 128, b0 + lc:b0 + lc + T], o[:, 0:T], ok)
                            return epi

                        def epi_tm(dstM, joff):
                            def epi(ji, j, si, ps, pk):
                                o, ok = og_r.next()
                                if si % 2:
                                    S.op('act', lambda a: a.copy(out=o[:, 0:128], in_=ps[:, 0:128]), reads=[pk], writes=[ok])
                                else:
                                    S.op('dve', lambda v: v.tensor_copy(out=o[:, 0:128], in_=ps[:, 0:128]), reads=[pk], writes=[ok])
                                store(dstM[b0 + si * 128:b0 + (si + 1) * 128, (j - joff) * 128:(j - joff + 1) * 128], o[:, 0:128], ok)
                            return epi

                        def jr(o0, width):
                            return list(range(o0 // 128, (o0 + width) // 128))
                        qk_epi = epi_rope if kind == 'lat' else epi_plain
                        ctx_last = (kind == 'ctx' and last)
                        if not ctx_last:
                            fm_linear(wn, jr(off['AQ'], BW), xs, 'ixs', tiles, qk_epi(qT, off['AQ'] // 128))
                        fm_linear(wn, jr(off['AK'], AKV), xs, 'ixs', tiles, qk_epi(kT, off['AK'] // 128))
                        tm_linear(wn, jr(off['AV'], AKV), xs, 'ixs', Tb, epi_tm(vA, off['AV'] // 128))
                        if not ctx_last:
                            fm_linear(wn, jr(off['NQ'], BW), xs, 'ixs', tiles, epi_plain(nqT, off['NQ'] // 128))
                        fm_linear(wn, jr(off['NK'], BW), xs, 'ixs', tiles, epi_plain(nkT, off['NK'] // 128))
                        tm_linear(wn, jr(off['NV'], BW), xs, 'ixs', Tb, epi_tm(vN, off['NV'] // 128))
                        if not ctx_last:
                            fm_linear(wn, jr(off['PU'], BW), xs, 'ixs', tiles, epi_plain(puT, off['PU'] // 128, f32=True))
                            tm_linear(wn, jr(off['FU'], BW), xs, 'ixs', Tb, epi_tm(fuM, off['FU'] // 128))
                S.barrier()

        def attn_phase(l, b, last, mode):
            with contextlib.ExitStack() as ph:
                nkv = HKV if mode == 'A' else NH
                KW = nkv * 128
                qsrc = qT if mode == 'A' else nqT
                ksrc = kT if mode == 'A' else nkT
                vsrc = vA if mode == 'A' else vN
                dst = ysT[0] if mode == 'A' else ysT[1]
                NLOC = 384 if mode == 'A' else 576
                NKB = (NLOC + 127) // 128
                LB = L // 128
                ckT = ph.enter_context(nc.sbuf_tensor(un("ckT"), [128, nkv, L], BF16))
                cV = ph.enter_context(nc.sbuf_tensor(un("cV"), [128, LB, KW], BF16))
                ld(ckT[:, :, :], ksrc[:, SQ:SQ + L].rearrange("(h p) t -> p h t", p=128), ['ckT'], 'ckT')
                ld(cV[:, :, :], vsrc[SQ:SQ + L, :].rearrange("(k p) c -> p k c", p=128), ['cV'], 'cV')
                q_r = Ring(nc, ph, "aq", [128, NH, 128], BF16, 2)
                k_r = Ring(nc, ph, "ak", [128, nkv, NLOC], BF16, 2)
                v_r = Ring(nc, ph, "av", [128, NKB, KW], BF16, 2)
                s_r = Ring(nc, ph, "as", [128, NLOC + L], F32, 2)
                p_r = Ring(nc, ph, "ap", [128, NLOC + L], BF16, 2)
                pt_r = Ring(nc, ph, "apt", [128, NKB + LB, 128], BF16, 2)
                b_r = Ring(nc, ph, "ab", [128, 576], F32, 2) if mode == 'N' else None
                st_r = Ring(nc, ph, "ast", [128, 8], F32, 4)
                o_r = Ring(nc, ph, "ao", [128, NH, 128], BF16, 2)
                jobs = []
                nblk = SQ // 128
                for n in range(nblk):
                    if mode == 'A':
                        kb0, kb1 = max(n - 1, 0), min(n + 1, nblk - 1)
                        jobs.append((n * 128, kb0 * 128, (kb1 - kb0 + 1) * 128, (kb0 - (n - 1)) * 128, None))
                    else:
                        base, var = na_tile_info[n]
                        jobs.append((n * 128, base * 64, 576, 0, var))
                if not last:
                    for n in range(LB):
                        jobs.append((SQ + n * 128, 0, 0, 0, None))
                for (q0, k0, nk, moff, var) in jobs:
                    qt, qk = q_r.next()
                    ld(qt[:, :, :], qsrc[:, q0:q0 + 128].rearrange("(h p) t -> p h t", p=128), [qk], 'aq%d' % qk[1])
                    nkb = (nk + 127) // 128
                    if nk:
                        kt, kk = k_r.next()
                        ld(kt[:, :, 0:nk], ksrc[:, k0:k0 + nk].rearrange("(h p) t -> p h t", p=128), [kk], 'ak%d' % kk[1])
                        vt, vk = v_r.next()
                        nfull = nk // 128
                        if nfull:
                            ld(vt[:, 0:nfull, :], vsrc[k0:k0 + nfull * 128, :].rearrange("(k p) c -> p k c", p=128), [vk], 'av%d' % vk[1])
                        if nk % 128:
                            ld(vt[0:nk % 128, nfull, :], vsrc[k0 + nfull * 128:k0 + nk, :], [vk], 'av%d' % vk[1])
                    ot, otk = o_r.next()
                    for h in range(NH):
                        kvh = h // G if mode == 'A' else h
                        tot = nk + L
                        s, sk = s_r.next()
                        if nk:
                            psA, pkA = psr.next()
                            for c0 in range(0, nk, 512):
                                c1 = min(nk, c0 + 512)
                                if c0 > 0:
                                    psA2, pkA2 = psr.next()
                                else:
                                    psA2, pkA2 = psA, pkA
                                S.op('pe', lambda p, c0=c0, c1=c1, psA2=psA2: p.matmul(psA2[:, 0:c1 - c0], qt[:, h, :], kt[:, kvh, c0:c1], start=True, stop=True),
                                     reads=[qk, kk], writes=[pkA2])
                                if mode == 'A':
                                    S.op('dve', lambda v, c0=c0, c1=c1, psA2=psA2: v.scalar_tensor_tensor(
                                        out=s[:, c0:c1], in0=psA2[:, 0:c1 - c0], scalar=scale, in1=wmask[:, moff + c0:moff + c1], op0=ALU.mult, op1=ALU.add),
                                        reads=[pkA2, 'wmask'], writes=[sk])
                                else:
                                    if c0 == 0:
                                        bt, bk = b_r.next()
                                        r0 = ((l * n_var + var) * NH + h) * 128
                                        ld(bt[:, :], nabias[r0:r0 + 128, :], [bk], 'ab%d' % bk[1])
                                    S.op('dve', lambda v, c0=c0, c1=c1, psA2=psA2: v.scalar_tensor_tensor(
                                        out=s[:, c0:c1], in0=psA2[:, 0:c1 - c0], scalar=scale, in1=bt[:, c0:c1], op0=ALU.mult, op1=ALU.add),
                                        reads=[pkA2, bk], writes=[sk])
                        psB, pkB = psr.next()
                        S.op('pe', lambda p: p.matmul(psB[:, 0:L], qt[:, h, :], ckT[:, kvh, :], start=True, stop=True), reads=[qk, 'ckT'], writes=[pkB])
                        S.op('act', lambda a: a.activation(out=s[:, nk:tot], in_=psB[:, 0:L], func=AF.Copy, scale=scale), reads=[pkB], writes=[sk])
                        stt, stk = st_r.next()
                        S.op('dve', lambda v: v.tensor_reduce(out=stt[:, 0:1], in_=s[:, 0:tot], axis=AX.X, op=ALU.max), reads=[sk], writes=[stk])
                        if mode == 'A':
                            S.op('dve', lambda v: v.tensor_tensor(out=stt[:, 0:1], in0=stt[:, 0:1], in1=sink_sb[:, h:h + 1], op=ALU.max), reads=[stk, 'sink'], writes=[stk])
                        S.op('dve', lambda v: v.tensor_scalar(out=stt[:, 1:2], in0=stt[:, 0:1], scalar1=-1.0, scalar2=None, op0=ALU.mult), reads=[stk], writes=[stk])
                        S.op('act', lambda a: a.activation(out=s[:, 0:tot], in_=s[:, 0:tot], func=AF.Exp, bias=stt[:, 1:2], scale=1.0, accum_out=stt[:, 2:3]),
                             reads=[sk, stk], writes=[sk, stk])
                        if mode == 'A':
                            S.op('act', lambda a: a.activation(out=stt[:, 3:4], in_=sink_sb[:, h:h + 1], func=AF.Exp, bias=stt[:, 1:2], scale=1.0), reads=['sink', stk], writes=[stk])
                            S.op('dve', lambda v: v.tensor_tensor(out=stt[:, 2:3], in0=stt[:, 2:3], in1=stt[:, 3:4], op=ALU.add), reads=[stk], writes=[stk])
                        S.op('dve', lambda v: v.reciprocal(out=stt[:, 4:5], in_=stt[:, 2:3]), reads=[stk], writes=[stk])
                        pp, ppk = p_r.next()
                        S.op('dve', lambda v: v.tensor_scalar(out=pp[:, 0:tot], in0=s[:, 0:tot], scalar1=stt[:, 4:5], scalar2=None, op0=ALU.mult), reads=[sk, stk], writes=[ppk])
                        ptt, ptk = pt_r.next()
                        blocks = []
                        for kb in range(nkb):
                            blocks.append((kb * 128, min(128, nk - kb * 128), 'loc', kb))
                        for kb in range(LB):
                            blocks.append((nk + kb * 128, 128, 'ctx', kb))
                        for bi, (c0, w_, kind, kb) in enumerate(blocks):
                            pT_, pTk = pst.next()
                            S.op('pe', lambda p, c0=c0, w_=w_, pT_=pT_: p.transpose(pT_[0:w_, 0:128], pp[:, c0:c0 + w_], identb[:, :]), reads=[ppk, 'identb'], writes=[pTk])
                            if bi % 2:
                                S.op('act', lambda a, bi=bi, w_=w_, pT_=pT_: a.copy(out=ptt[0:w_, bi, :], in_=pT_[0:w_, 0:128]), reads=[pTk], writes=[ptk])
                            else:
                                S.op('dve', lambda v, bi=bi, w_=w_, pT_=pT_: v.tensor_copy(out=ptt[0:w_, bi, :], in_=pT_[0:w_, 0:128]), reads=[pTk], writes=[ptk])
                        psO, pkO = psr.next()
                        for bi, (c0, w_, kind, kb) in enumerate(blocks):
                            if kind == 'loc':
                                S.op('pe', lambda p, bi=bi, w_=w_, kb=kb: p.matmul(psO[:, 0:128], vt[0:w_, kb, kvh * 128:(kvh + 1) * 128], ptt[0:w_, bi, :],
                                                                               start=(bi == 0), stop=(bi == len(blocks) - 1)), reads=[vk, ptk], writes=[pkO])
                            else:
                                S.op('pe', lambda p, bi=bi, kb=kb: p.matmul(psO[:, 0:128], cV[:, kb, kvh * 128:(kvh + 1) * 128], ptt[:, bi, :],
                                                                        start=(bi == 0), stop=(bi == len(blocks) - 1)), reads=['cV', ptk], writes=[pkO])
                        S.op('act', lambda a: a.copy(out=ot[:, h, :], in_=psO[:, 0:128]), reads=[pkO], writes=[otk])
                    ld(dst[:, q0:q0 + 128].rearrange("(h p) t -> p h t", p=128), ot[:, :, :], [], 'aost', reads=[otk])
                S.barrier()

        def pool_phase(l, b, last):
            with contextlib.ExitStack() as ph:
                CW = 2048
                up_r = Ring(nc, ph, "pu", [128, CW + 16], F32, 2)
                a_r = Ring(nc, ph, "pa", [128, CW + 16], F32, 2)
                ic_r = Ring(nc, ph, "pic", [128, CW], F32, 2)
                dT = ph.enter_context(nc.sbuf_tensor(un("pdT"), [128, PGC, CW], BF16))
                og_r = Ring(nc, ph, "pog", [128, 512], BF16, 3)
                for (s0, n, kind) in segs(last):
                    icd = invcnt_d if kind == 'lat' else invcntc_d
                    for c0 in range(0, n, CW):
                        cw = min(CW, n - c0)
                        for g in range(4):
                            w = POOL_WINDOWS[g]
                            ic, ick = ic_r.next()
                            ld(ic[:, 0:cw], icd[g * 128:(g + 1) * 128, c0:c0 + cw], [ick], 'pic%d' % ick[1])
                            for cc in range(PGC):
                                ch = g * PGC + cc
                                up, uk = up_r.next()
                                lo = max(c0 - 8, 0)
                                hi = min(c0 + cw + 8, n)
                                if lo > c0 - 8:
                                    S.op('pool', lambda v: v.memset(up[:, 0:8], 0.0), writes=[uk])
                                if hi < c0 + cw + 8:
                                    S.op('pool', lambda v: v.memset(up[:, 8 + cw:16 + cw], 0.0), writes=[uk])
                                ld(up[:, 8 - (c0 - lo):8 + (hi - c0)], puT[ch * 128:(ch + 1) * 128, s0 + lo:s0 + hi], [uk], 'pu%d' % uk[1])
                                cur, ck = up, uk
                                ln = cw + 16
                                step = 1
                                while step < w:
                                    nx, nk_ = a_r.next()
                                    ln -= step
                                    S.op('dve', lambda v, cur=cur, nx=nx, ln=ln, step=step: v.tensor_tensor(out=nx[:, 0:ln], in0=cur[:, 0:ln], in1=cur[:, step:step + ln], op=ALU.add),
                                         reads=[ck], writes=[nk_])
                                    cur, ck = nx, nk_
                                    step *= 2
                                o0 = 8 - w // 2
                                nx, nk_ = a_r.next()
                                S.op('dve', lambda v, cur=cur, nx=nx: v.tensor_tensor(out=nx[:, 0:cw], in0=cur[:, o0:o0 + cw], in1=ic[:, 0:cw], op=ALU.mult), reads=[ck, ick], writes=[nk_])
                                S.op('dve', lambda v, nx=nx, cc=cc: v.tensor_tensor(out=dT[:, cc, 0:cw], in0=nx[:, 0:cw], in1=up[:, 8:8 + cw], op=ALU.subtract), reads=[nk_, uk], writes=['pdT'])
                            tiles = col_tiles(0, cw, 512)

                            def epi(ji, j, ti, T, ps, pk):
                                lc = tiles[ti][0]
                                o, ok = og_r.next()
                                S.op('dve', lambda v: v.tensor_scalar(out=o[:, 0:T], in0=ps[:, 0:T], scalar1=pscale_sb[:, g * PGC + j:g * PGC + j + 1], scalar2=None, op0=ALU.mult),
                                     reads=[pk, 'pscale'], writes=[ok])
                                ld(ysT[2][(g * PGC + j) * 128:(g * PGC + j + 1) * 128, s0 + c0 + lc:s0 + c0 + lc + T], o[:, 0:T], [], 'post', reads=[ok])
                            fm_linear("pool%d_%d" % (l, g), list(range(PGC)), dT, 'pdT', tiles, epi)
                S.barrier()

        def fnet_phase(l, b, last):
            with contextlib.ExitStack() as ph:
                NBmax = SQ // 128
                U = ph.enter_context(nc.sbuf_tensor(un("fU"), [128, NBmax, PG], BF16))
                gc = ph.enter_context(nc.sbuf_tensor(un("fgc"), [128, PGC, PG], BF16))
                gs = ph.enter_context(nc.sbuf_tensor(un("fgs"), [128, PGC, PG], BF16))
                ld(gc[:, :, :], dftgc_d.rearrange("(k p) n -> p k n", p=128), ['fgc'], 'fgc')
                ld(gs[:, :, :], dftgs_d.rearrange("(k p) n -> p k n", p=128), ['fgs'], 'fgs')
                d_r = Ring(nc, ph, "fd", [128, 2, 4, 512], BF16, 3)
                pq_r = Ring(nc, ph, "fpq", [128, 2, PGC, 512], BF16, 2)
                og_r = Ring(nc, ph, "fog", [128, 512], BF16, 3)
                for (s0, n, kind) in segs(last):
                    nbn = n // 128
                    sc = float(1.0 / np.sqrt(float(n) * PG))
                    dC, dS = dft_d[('c', n)], dft_d[('s', n)]
                    for g in range(4):
                        ld(U[:, 0:nbn, :], fuM[s0:s0 + n, g * PG:(g + 1) * PG].rearrange("(k p) c -> p k c", p=128), ['fU'], 'fU')
                        for (k0, T) in col_tiles(0, n, 512):
                            pss = [[psr.next() for _ in range(PGC)] for _ in range(2)]
                            for nb0 in range(0, nbn, 4):
                                nb1 = min(nbn, nb0 + 4)
                                dt_, dk = d_r.next()
                                ld(dt_[:, 0, 0:nb1 - nb0, 0:T], dC[nb0 * 128:nb1 * 128, k0:k0 + T].rearrange("(k p) t -> p k t", p=128), [dk], 'fd%d' % dk[1])
                                ld(dt_[:, 1, 0:nb1 - nb0, 0:T], dS[nb0 * 128:nb1 * 128, k0:k0 + T].rearrange("(k p) t -> p k t", p=128), [dk], 'fd%d' % dk[1])
                                for nb in range(nb0, nb1):
                                    for cs in range(2):
                                        for cc in range(PGC):
                                            ps, pk = pss[cs][cc]
                                            S.op('pe', lambda p, ps=ps, nb=nb, cs=cs, cc=cc, dt_=dt_, nb0=nb0: p.matmul(
                                                ps[:, 0:T], U[:, nb, cc * 128:(cc + 1) * 128], dt_[:, cs, nb - nb0, 0:T], start=(nb == 0), stop=(nb == nbn - 1)),
                                                reads=['fU', dk], writes=[pk])
                            pq, pqk = pq_r.next()
                            for cs in range(2):
                                for cc in range(PGC):
                                    ps, pk = pss[cs][cc]
                                    S.op('act', lambda a, ps=ps, cs=cs, cc=cc: a.activation(out=pq[:, cs, cc, 0:T], in_=ps[:, 0:T], func=AF.Copy, scale=(1.0 if cs == 0 else -1.0)),
                                         reads=[pk], writes=[pqk])
                            for mc in range(PGC):
                                ps, pk = psr.next()
                                i_ = 0
                                for cs in range(2):
                                    gm = gc if cs == 0 else gs
                                    gk = 'fgc' if cs == 0 else 'fgs'
                                    for cc in range(PGC):
                                        S.op('pe', lambda p, gm=gm, cs=cs, cc=cc, i_=i_: p.matmul(ps[:, 0:T], gm[:, cc, mc * 128:(mc + 1) * 128], pq[:, cs, cc, 0:T],
                                                                                       start=(i_ == 0), stop=(i_ == 2 * PGC - 1)), reads=[gk, pqk], writes=[pk])
                                        i_ += 1
                                o, ok = og_r.next()
                                S.op('act', lambda a: a.activation(out=o[:, 0:T], in_=ps[:, 0:T], func=AF.Copy, scale=sc), reads=[pk], writes=[ok])
                                ld(yfT[(g * PGC + mc) * 128:(g * PGC + mc + 1) * 128, s0 + k0:s0 + k0 + T], o[:, 0:T], [], 'fost', reads=[ok])
                S.barrier()
                TB = 1024
                xs = ph.enter_context(nc.sbuf_tensor(un("fxs"), [128, BC, TB], BF16))
                for (s0, n, kind) in segs(last):
                    for (b0, Tb) in col_tiles(s0, n, TB):
                        ld(xs[:, :, 0:Tb], yfT[:, b0:b0 + Tb].rearrange("(k p) t -> p k t", p=128), ['fxs'], 'fxs')
                        tiles = col_tiles(0, Tb, 512)

                        def epi(ji, j, ti, T, ps, pk):
                            lc = tiles[ti][0]
                            o, ok = og_r.next()
                            S.op('act', lambda a: a.copy(out=o[:, 0:T], in_=ps[:, 0:T]), reads=[pk], writes=[ok])
                            ld(ysT[3][j * 128:(j + 1) * 128, b0 + lc:b0 + lc + T], o[:, 0:T], [], 'fost2', reads=[ok])
                        fm_linear("fnet%d" % l, list(range(BC)), xs, 'fxs', tiles, epi)
                S.barrier()

        def merge_phase(l, b, last):
            with contextlib.ExitStack() as ph:
                TB = 512
                hs = ph.enter_context(nc.sbuf_tensor(un("mhs"), [128, KC, TB], BF16))
                ys = ph.enter_context(nc.sbuf_tensor(un("mys"), [128, 4, BC, TB], BF16))
                g_r = Ring(nc, ph, "mg", [128, TB], F32, 2)
                acc_r = Ring(nc, ph, "macc", [128, TB], F32, 2)
                tmp_r = Ring(nc, ph, "mtmp", [128, TB], F32, 2)
                og_r = Ring(nc, ph, "mog", [128, TB], BF16, 2)
                wn = "in%d" % l
                for (s0, n, kind) in segs(last):
                    for (b0, T) in col_tiles(s0, n, TB):
                        ld(hs[:, :, 0:T], hT[:, b0:b0 + T].rearrange("(k p) t -> p k t", p=128), ['mhs'], 'mhs')
                        for i in range(4):
                            ld(ys[:, i, :, 0:T], ysT[i][:, b0:b0 + T].rearrange("(k p) t -> p k t", p=128), ['mys'], 'mys')
                        for j in range(KC):
                            acc, acck = acc_r.next()
                            for i in cfg.get('branches', (0, 1, 2, 3)):
                                w, wk, kcw = load_w(wn, (off['GT'] + i * D) // 128 + j)
                                w2, wk2, kcw2 = load_w("br%d_%d" % (l, i), j)
                                psg, pkg = psr.next()
                                S.chain('pe', [lambda p, kc=kc: p.matmul(psg[:, 0:T], w[:, kc * 128:(kc + 1) * 128], hs[:, kc, 0:T], start=(kc == 0), stop=(kc == KC - 1))
                                               for kc in range(KC)], reads=[wk, 'mhs'], writes=[pkg])
                                psb, pkb = psr.next()
                                S.chain('pe', [lambda p, kc=kc: p.matmul(psb[:, 0:T], w2[:, kc * 128:(kc + 1) * 128], ys[:, i, kc, 0:T], start=(kc == 0), stop=(kc == BC - 1))
                                               for kc in range(BC)], reads=[wk2, 'mys'], writes=[pkb])
                                gt_, gk = g_r.next()
                                S.op('act', lambda a: a.activation(out=gt_[:, 0:T], in_=psg[:, 0:T], func=AF.Sigmoid), reads=[pkg], writes=[gk])
                                if i == cfg.get('branches', (0, 1, 2, 3))[0]:
                                    S.op('dve', lambda v: v.tensor_tensor(out=acc[:, 0:T], in0=psb[:, 0:T], in1=gt_[:, 0:T], op=ALU.mult), reads=[pkb, gk], writes=[acck])
                                else:
                                    tm, tmk = tmp_r.next()
                                    S.op('dve', lambda v: v.tensor_tensor(out=tm[:, 0:T], in0=psb[:, 0:T], in1=gt_[:, 0:T], op=ALU.mult), reads=[pkb, gk], writes=[tmk])
                                    S.op('pool', lambda v: v.tensor_tensor(out=acc[:, 0:T], in0=acc[:, 0:T], in1=tm[:, 0:T], op=ALU.add), reads=[tmk, acck], writes=[acck])
                            o, ok = og_r.next()
                            S.op('act', lambda a: a.copy(out=o[:, 0:T], in_=acc[:, 0:T]), reads=[acck], writes=[ok])
                            ld(mT[j * 128:(j + 1) * 128, b0:b0 + T], o[:, 0:T], [], 'most', reads=[ok])
                S.barrier()

        def resid_phase(l, b, last, wname, src, Kc, gidx, TB):
            with contextlib.ExitStack() as ph:
                xs = ph.enter_context(nc.sbuf_tensor(un("rxs"), [128, Kc, TB], BF16))
                xr_r = Ring(nc, ph, "rxr", [128, 512], F32, 3)
                of_r = Ring(nc, ph, "rof", [128, 512], F32, 3)
                for (s0, n, kind) in segs(last):
                    row = b if kind == 'lat' else NB
                    for (b0, Tb) in col_tiles(s0, n, TB):
                        ld(xs[:, :, 0:Tb], src[:, b0:b0 + Tb].rearrange("(k p) t -> p k t", p=128), ['rxs'], 'rxs')
                        tiles = col_tiles(0, Tb, 512)

                        def epi(ji, j, ti, T, ps, pk):
                            lc = tiles[ti][0]
                            xr, xrk = xr_r.next()
                            ld(xr[:, 0:T], xT[j * 128:(j + 1) * 128, b0 + lc:b0 + lc + T], [xrk], 'rxr%d' % xrk[1])
                            o, ok = of_r.next()
                            S.op('dve', lambda v: v.scalar_tensor_tensor(out=o[:, 0:T], in0=ps[:, 0:T], scalar=mod_sb[:, gidx * KC + j, row:row + 1],
                                                                         in1=xr[:, 0:T], op0=ALU.mult, op1=ALU.add), reads=[pk, xrk, 'mod'], writes=[ok])
                            ld(xT[j * 128:(j + 1) * 128, b0 + lc:b0 + lc + T], o[:, 0:T], [], 'rost', reads=[ok])
                        fm_linear(wname, list(range(KC)), xs, 'rxs', tiles, epi)
                S.barrier()

        def ffn_phase(l, b, last):
            with contextlib.ExitStack() as ph:
                TO = 510
                NT = 2
                XW = NT * TO + 2
                xs = ph.enter_context(nc.sbuf_tensor(un("gxs"), [128, KC, XW], BF16))
                u_r = Ring(nc, ph, "gu", [128, TO], F32, 3)
                sg_r = Ring(nc, ph, "gsg", [128, TO], F32, 2)
                og_r = Ring(nc, ph, "gog", [128, TO], BF16, 3)
                for (s0, n, kind) in segs(last):
                    for b0 in range(0, n, NT * TO):
                        nb_ = min(NT * TO, n - b0)
                        lo, hi = b0 - 1, b0 + nb_ + 1
                        clo, chi = max(lo, 0), min(hi, n)
                        if clo > lo:
                            S.op('pool', lambda v: v.memset(xs[:, :, 0:1], 0.0), writes=['gxs'])
                        if chi < hi:
                            S.op('pool', lambda v, e_=hi - lo: v.memset(xs[:, :, e_ - 1:e_], 0.0), writes=['gxs'])
                        ld(xs[:, :, clo - lo:chi - lo], hT[:, s0 + clo:s0 + chi].rearrange("(k p) t -> p k t", p=128), ['gxs'], 'gxs')
                        touts = [(t0, min(TO, nb_ - t0)) for t0 in range(0, nb_, TO)]
                        nxt = (load_w("fg%d" % l, 0), load_w("fv%d" % l, 0))
                        for j in range(FC):
                            (wg, wgk, _), (wv, wvk, _) = nxt
                            if j + 1 < FC:
                                nxt = (load_w("fg%d" % l, j + 1), load_w("fv%d" % l, j + 1))
                            for (t0, To) in touts:
                                Tc = To + 2
                                psa, pka = psr.next()
                                S.chain('pe', [lambda p, kc=kc: p.matmul(psa[:, 0:Tc], wg[:, kc * 128:(kc + 1) * 128], xs[:, kc, t0:t0 + Tc], start=(kc == 0), stop=(kc == KC - 1))
                                               for kc in range(KC)], reads=[wgk, 'gxs'], writes=[pka])
                                psv, pkv = psr.next()
                                S.chain('pe', [lambda p, kc=kc: p.matmul(psv[:, 0:To], wv[:, kc * 128:(kc + 1) * 128], xs[:, kc, t0 + 1:t0 + 1 + To], start=(kc == 0), stop=(kc == KC - 1))
                                               for kc in range(KC)], reads=[wvk, 'gxs'], writes=[pkv])
                                u0, u0k = u_r.next()
                                S.op('dve', lambda v: v.tensor_scalar(out=u0[:, 0:To], in0=psa[:, 1:1 + To], scalar1=convw_sb[:, 1, j:j + 1], scalar2=convw_sb[:, 3, j:j + 1], op0=ALU.mult, op1=ALU.add),
                                     reads=[pka, 'convw'], writes=[u0k])
                                u1, u1k = u_r.next()
                                S.op('dve', lambda v: v.scalar_tensor_tensor(out=u1[:, 0:To], in0=psa[:, 0:To], scalar=convw_sb[:, 0, j:j + 1], in1=u0[:, 0:To], op0=ALU.mult, op1=ALU.add),
                                     reads=[pka, u0k, 'convw'], writes=[u1k])
                                u2, u2k = u_r.next()
                                S.op('dve', lambda v: v.scalar_tensor_tensor(out=u2[:, 0:To], in0=psa[:, 2:2 + To], scalar=convw_sb[:, 2, j:j + 1], in1=u1[:, 0:To], op0=ALU.mult, op1=ALU.add),
                                     reads=[pka, u1k, 'convw'], writes=[u2k])
                                sg, sgk = sg_r.next()
                                S.op('act', lambda a: a.activation(out=sg[:, 0:To], in_=u2[:, 0:To], func=AF.Silu), reads=[u2k], writes=[sgk])
                                o, ok = og_r.next()
                                S.op('dve', lambda v: v.tensor_tensor(out=o[:, 0:To], in0=sg[:, 0:To], in1=psv[:, 0:To], op=ALU.mult), reads=[sgk, pkv], writes=[ok])
                                ld(pT[j * 128:(j + 1) * 128, s0 + b0 + t0:s0 + b0 + t0 + To], o[:, 0:To], [], 'gost', reads=[ok])
                S.barrier()

        S.barrier()
        for b in range(NB):
            load_x_phase(b)
            for l in range(DEPTH):
                last = (l == DEPTH - 1)
                mod_phase(l)
                if not cfg.get('skip_mixer'):
                    norm_phase(b, 0, last)
                    inproj_phase(l, b, last)
                    attn_phase(l, b, last, 'A')
                    attn_phase(l, b, last, 'N')
                    pool_phase(l, b, last)
                    fnet_phase(l, b, last)
                    merge_phase(l, b, last)
                    resid_phase(l, b, last, "out%d" % l, mT, KC, 2, 512)
                if not cfg.get('skip_ffn'):
                    norm_phase(b, 1, last)
                    ffn_phase(l, b, last)
                    resid_phase(l, b, last, "fd%d" % l, pT, FC, 5, 512)
            norm_phase(b, 0, True, final=True)
        S.barrier()
    return nc


def prep_inputs(cfg, inputs, cores):
    NB, SQ, L, D, DEPTH = cfg['NB'], cfg['SQ'], cfg['L'], cfg['D'], cfg['DEPTH']
    f = lambda a: np.ascontiguousarray(np.asarray(a, dtype=np.float32))
    consts, _ = host_consts(cfg, None)
    consts = {k: v for k, v in consts.items() if not k.startswith('_')}
    nab, tiles = build_nabias(cfg, f(inputs['na_rpb']))
    shared = {
        'c_ctx': f(inputs['c_ctx']).reshape(1, D),
        'w_mod': f(inputs['w_mod']).reshape(DEPTH * D, 6 * D),
        'b_mod': f(inputs['b_mod']), 'g_mix': f(inputs['g_mix']),
        'w_in': f(inputs['w_in']).reshape(DEPTH * D, -1),
        'a_sink': f(inputs['a_sink']),
        'nabias': nab.reshape(-1, 576),
        'w_pool': f(inputs['w_pool']).reshape(-1, cfg['PG']),
        'pool_scale': f(inputs['pool_scale']),
        'w_fnet': f(inputs['w_fnet']).reshape(-1, cfg['BW']),
        'w_branch': f(inputs['w_branch']).reshape(-1, D),
        'w_out': f(inputs['w_out']).reshape(-1, D),
        'g_ffn': f(inputs['g_ffn']),
        'w_ff_gate': f(inputs['w_ff_gate']).reshape(-1, cfg['DFF']),
        'w_ff_val': f(inputs['w_ff_val']).reshape(-1, cfg['DFF']),
        'ff_conv_w': f(inputs['ff_conv_w']).reshape(-1, cfg['DFF']),
        'ff_conv_b': f(inputs['ff_conv_b']),
        'w_ff_down': f(inputs['w_ff_down']).reshape(-1, D),
        'g_final': f(inputs['g_final']).reshape(1, D),
        'invcnt': consts.pop('invcnt').reshape(4 * 128, SQ),
        'invcntc': consts.pop('invcntc').reshape(4 * 128, L),
    }
    shared.update(consts)
    maps = []
    x, c, ctx = f(inputs['x']), f(inputs['c']), f(inputs['ctx'])
    for ci in range(cores):
        m = dict(shared)
        m['x'] = x[ci * NB:(ci + 1) * NB].reshape(NB * SQ, D)
        m['c'] = c[ci * NB:(ci + 1) * NB]
        m['ctx'] = ctx[ci * NB:(ci + 1) * NB].reshape(NB * L, D)
        maps.append(m)
    return maps, tiles, nab.shape[1]


def run(cfg, inputs, cores):
    maps, tiles, n_var = prep_inputs(cfg, inputs, cores)
    nc = build(cfg, tiles, n_var)
    res = run_bass_kernel_spmd(nc, maps, core_ids=list(range(cores)))
    SQ, D, NB = cfg['SQ'], cfg['D'], cfg['NB']
    return np.concatenate([np.asarray(r['y']).reshape(NB, SQ, D) for r in res.results], axis=0).astype(np.float32)


def kernel(**inputs):
    cores = 2
    cfg = make_cfg(D=4096, SQ=8192, L=256, DEPTH=2, NB=2 // cores)
    return run(cfg, inputs, cores)
```
